# Optimizing a Trainium2 kernel written in Bass

```python
import math
import jax, jax.numpy as jnp
from jax import lax
import numpy as np

D_MODEL = 4096
BATCH = 1
SEQ = 8192
DEPTH = 4

D_SSD = 3 * D_MODEL // 4
SSD_HEAD_DIM = 64
SSD_HEADS = D_SSD // SSD_HEAD_DIM
SSD_GROUPS = 8
SSD_STATE = 128
SSD_CONV = 4
CHUNK = 128
SSD_GN = SSD_GROUPS * SSD_STATE
SSD_CONV_DIM = D_SSD + 2 * SSD_GN
D_SC = 3 * D_MODEL // 8
SC_GROUP_DIM = 128
SC_GROUPS = D_SC // SC_GROUP_DIM
SC_CONV = 3
N_BRANCH = 2
GATE_BLOCKS = 16
GATE_BLOCK_DIM = D_MODEL // GATE_BLOCKS
D_FF = ((8 * D_MODEL + 3 * 256 - 1) // (3 * 256)) * 256
COND_RANK = D_MODEL // 16
N_MOD = 6
EPS = 1e-6
IN_COLS = D_SSD + SSD_CONV_DIM + SSD_HEADS + 3 * D_SC

kernel_name = "hybrid_ssd_shortconv_gated_adaln_trunk"


def rms_norm(x, w):
    x32 = x.astype(jnp.float32)
    y = x32 * lax.rsqrt(jnp.mean(x32 * x32, axis=-1, keepdims=True) + EPS)
    return (y * w.astype(jnp.float32)).astype(x.dtype)


def modulate(h, shift, scale):
    return h * (1 + scale) + shift


def causal_depthwise_conv(u, w):
    k = w.shape[0]
    return lax.conv_general_dilated(
        u, w[:, None, :].astype(u.dtype), window_strides=(1,), padding=[(k - 1, 0)],
        dimension_numbers=("NWC", "WIO", "NWC"), feature_group_count=u.shape[-1])


def ssd_chunked_scan(x, dt, a, bm, cm):
    b, l, h, p = x.shape
    g, n = bm.shape[-2:]
    r = h // g
    nc = l // CHUNK
    xd = (x * dt[..., None]).reshape(b, nc, CHUNK, g, r, p)
    da = (dt * a).reshape(b, nc, CHUNK, g, r)
    bc = bm.reshape(b, nc, CHUNK, g, n)
    cc = cm.reshape(b, nc, CHUNK, g, n)
    a_cs = jnp.cumsum(da, axis=2)
    acs_t = jnp.moveaxis(a_cs, 2, -1)
    seg = acs_t[..., :, None] - acs_t[..., None, :]
    causal = jnp.tril(jnp.ones((CHUNK, CHUNK), dtype=bool))
    decay = jnp.exp(jnp.where(causal, seg, -jnp.inf))
    cb = jnp.einsum("bclgn,bcsgn->bcgls", cc, bc)
    y_diag = jnp.einsum("bcgrls,bcsgrp->bclgrp", cb[:, :, :, None] * decay, xd)
    decay_to_end = jnp.exp(a_cs[:, :, -1:] - a_cs)
    states = jnp.einsum("bcsgn,bcsgr,bcsgrp->bcgrpn", bc, decay_to_end, xd)
    chunk_decay = jnp.exp(a_cs[:, :, -1])

    def step(hs, inp):
        st, dec = inp
        return hs * dec[..., None, None] + st, hs

    h0 = jnp.zeros_like(states[:, 0])
    _, prev = lax.scan(step, h0, (jnp.moveaxis(states, 1, 0), jnp.moveaxis(chunk_decay, 1, 0)))
    prev = jnp.moveaxis(prev, 0, 1)
    y_off = jnp.einsum("bclgn,bcgrpn,bclgr->bclgrp", cc, prev, jnp.exp(a_cs))
    return (y_diag + y_off).reshape(b, l, h, p)


def hybrid_mixer(h, w_in, conv_ssd_w, conv_ssd_b, dt_bias, a_log, d_skip, ssd_norm_w,
                 sc_conv_w, w_gate, b_gate, w_br_ssd, w_br_sc, w_o):
    b, l, _ = h.shape
    proj = h @ w_in
    z, xbc, dt_raw, sc_in = jnp.split(
        proj, [D_SSD, D_SSD + SSD_CONV_DIM, D_SSD + SSD_CONV_DIM + SSD_HEADS], axis=-1)
    xbc = jax.nn.silu(causal_depthwise_conv(xbc, conv_ssd_w) + conv_ssd_b)
    xs, bm, cm = jnp.split(xbc, [D_SSD, D_SSD + SSD_GN], axis=-1)
    xs = xs.reshape(b, l, SSD_HEADS, SSD_HEAD_DIM)
    bm = bm.reshape(b, l, SSD_GROUPS, SSD_STATE)
    cm = cm.reshape(b, l, SSD_GROUPS, SSD_STATE)
    dt = jax.nn.softplus(dt_raw.astype(jnp.float32) + dt_bias.astype(jnp.float32))
    a = -jnp.exp(a_log.astype(jnp.float32))
    y = ssd_chunked_scan(xs, dt, a, bm, cm) + d_skip[:, None] * xs
    y = y.reshape(b, l, D_SSD).astype(h.dtype) * jax.nn.silu(z)
    y = rms_norm(y.reshape(b, l, SSD_GROUPS, D_SSD // SSD_GROUPS),
                 ssd_norm_w.reshape(SSD_GROUPS, D_SSD // SSD_GROUPS)).reshape(b, l, D_SSD)
    y_a = y @ w_br_ssd
    v, bg, cg = jnp.split(sc_in, 3, axis=-1)
    y_b = (bg * causal_depthwise_conv(cg * v, sc_conv_w)) @ w_br_sc
    hb = h.reshape(b, l, GATE_BLOCKS, GATE_BLOCK_DIM)
    gates = jax.nn.sigmoid(
        jnp.einsum("blkd,gkde->blgke", hb, w_gate).reshape(b, l, N_BRANCH, D_MODEL) + b_gate)
    merged = gates[:, :, 0] * y_a + gates[:, :, 1] * y_b
    return merged @ w_o


def swiglu(h, w_ffn_in, w_ffn_out):
    g, u = jnp.split(h @ w_ffn_in, 2, axis=-1)
    return (jax.nn.silu(g) * u) @ w_ffn_out


def setup_inputs(seed: int = 0) -> dict:
    key = jax.random.key(seed)
    ks = jax.random.split(key, 24)
    f32 = jnp.float32
    nrm = lambda k, shape, s: jax.random.normal(k, shape, f32) * s
    dt0 = jnp.exp(jax.random.uniform(ks[8], (DEPTH, SSD_HEADS), f32) * (math.log(0.1) - math.log(0.001))
                  + math.log(0.001))
    return {
        "x": nrm(ks[0], (BATCH, SEQ, D_MODEL), 1.0),
        "c": nrm(ks[1], (BATCH, D_MODEL), 1.0),
        "w_cond": nrm(ks[2], (D_MODEL, COND_RANK), D_MODEL ** -0.5),
        "b_cond": nrm(ks[3], (COND_RANK,), 0.02),
        "w_mod": nrm(ks[4], (DEPTH, COND_RANK, N_MOD * D_MODEL), COND_RANK ** -0.5),
        "b_mod": nrm(ks[5], (DEPTH, N_MOD * D_MODEL), 0.02),
        "norm_mix_w": 1.0 + nrm(ks[6], (DEPTH, D_MODEL), 0.02),
        "w_in": nrm(ks[7], (DEPTH, D_MODEL, IN_COLS), D_MODEL ** -0.5),
        "conv_ssd_w": nrm(ks[9], (DEPTH, SSD_CONV, SSD_CONV_DIM), SSD_CONV ** -0.5),
        "conv_ssd_b": nrm(ks[10], (DEPTH, SSD_CONV_DIM), 0.02),
        "dt_bias": dt0 + jnp.log(-jnp.expm1(-dt0)),
        "a_log": jnp.log(jax.random.uniform(ks[11], (DEPTH, SSD_HEADS), f32, 1.0, 16.0)),
        "d_skip": 1.0 + nrm(ks[12], (DEPTH, SSD_HEADS), 0.02),
        "ssd_norm_w": 1.0 + nrm(ks[13], (DEPTH, D_SSD), 0.02),
        "sc_conv_w": nrm(ks[14], (DEPTH, SC_CONV, D_SC), SC_CONV ** -0.5),
        "w_gate": nrm(ks[15], (DEPTH, N_BRANCH, GATE_BLOCKS, GATE_BLOCK_DIM, GATE_BLOCK_DIM),
                      GATE_BLOCK_DIM ** -0.5),
        "b_gate": nrm(ks[16], (DEPTH, N_BRANCH, D_MODEL), 0.02),
        "w_br_ssd": nrm(ks[17], (DEPTH, D_SSD, D_MODEL), D_SSD ** -0.5),
        "w_br_sc": nrm(ks[18], (DEPTH, D_SC, D_MODEL), D_SC ** -0.5),
        "w_o": nrm(ks[19], (DEPTH, D_MODEL, D_MODEL), D_MODEL ** -0.5),
        "norm_ffn_w": 1.0 + nrm(ks[20], (DEPTH, D_MODEL), 0.02),
        "w_ffn_in": nrm(ks[21], (DEPTH, D_MODEL, 2 * D_FF), D_MODEL ** -0.5),
        "w_ffn_out": nrm(ks[22], (DEPTH, D_FF, D_MODEL), D_FF ** -0.5),
        "final_norm_w": 1.0 + nrm(ks[23], (D_MODEL,), 0.02),
    }


def reference(x, c, w_cond, b_cond, w_mod, b_mod, norm_mix_w, w_in, conv_ssd_w, conv_ssd_b,
              dt_bias, a_log, d_skip, ssd_norm_w, sc_conv_w, w_gate, b_gate, w_br_ssd, w_br_sc,
              w_o, norm_ffn_w, w_ffn_in, w_ffn_out, final_norm_w):
    b = x.shape[0]
    t = jax.nn.silu(c) @ w_cond + b_cond
    for i in range(DEPTH):
        mod = (t @ w_mod[i] + b_mod[i]).reshape(b, N_MOD, D_MODEL)
        sh_m, sc_m, g_m, sh_f, sc_f, g_f = [mod[:, j, None, :] for j in range(N_MOD)]
        h = modulate(rms_norm(x, norm_mix_w[i]), sh_m, sc_m)
        x = x + g_m * hybrid_mixer(h, w_in[i], conv_ssd_w[i], conv_ssd_b[i], dt_bias[i], a_log[i],
                                   d_skip[i], ssd_norm_w[i], sc_conv_w[i], w_gate[i], b_gate[i],
                                   w_br_ssd[i], w_br_sc[i], w_o[i])
        h = modulate(rms_norm(x, norm_ffn_w[i]), sh_f, sc_f)
        x = x + g_f * swiglu(h, w_ffn_in[i], w_ffn_out[i])
    return rms_norm(x, final_norm_w)
```

```python
import numpy as np
import concourse.bass as bass
import concourse.mybir as mybir
from concourse.bass_utils import run_bass_kernel_spmd

F32 = mybir.dt.float32
BF16 = mybir.dt.bfloat16
AF = mybir.ActivationFunctionType
ALU = mybir.AluOpType
NCORES = 8
HP = 16
EPS = 1e-6
SB_LO = 16640
SB_HI = 229376


def _esz(dt):
    return 2 if dt == BF16 else 4


class Cfg:
    def __init__(s, D=4096, SEQ=8192, DEPTH=4, GROUPS=8, GATE_BLOCKS=16, RANK=256):
        s.D, s.SEQ, s.DEPTH, s.G, s.GB, s.RANK = D, SEQ, DEPTH, GROUPS, GATE_BLOCKS, RANK
        s.T = SEQ // NCORES
        s.DSSD = 3 * D // 4
        s.HEADS = s.DSSD // 64
        s.R = s.HEADS // GROUPS
        assert s.R == 6
        s.GN = GROUPS * 128
        s.CONVD = s.DSSD + 2 * s.GN
        s.DSC = 3 * D // 8
        s.GBD = D // GATE_BLOCKS
        assert s.GBD == 256
        s.DFF = ((8 * D + 3 * 256 - 1) // (3 * 256)) * 256
        s.INC = s.DSSD + s.CONVD + s.HEADS + 3 * s.DSC
        s.KC = D // 128
        s.RC = RANK // 128
        s.CC = s.CONVD // 128
        s.SCC = s.DSC // 128
        s.YC = s.DSSD // 128
        s.FC = s.DFF // 128
        s.NT = min(512, s.T)
        s.NH = s.T // s.NT
        s.NCH = s.T // 128
        s.TP = s.T + HP
        s.OZ = 0
        s.OX = s.DSSD
        s.OB = 2 * s.DSSD
        s.OC = s.OB + s.GN
        s.ODT = s.DSSD + s.CONVD
        s.OV = s.ODT + s.HEADS
        s.OBG = s.OV + s.DSC
        s.OCG = s.OBG + s.DSC
        KC, CC, SCC = s.KC, s.CC, s.SCC
        s.PC_BMOD = 0
        s.PC_NMW = 6 * KC
        s.PC_NFW = 7 * KC
        s.PC_BG = 8 * KC
        s.PC_CW = 10 * KC
        s.PC_CB = s.PC_CW + 4 * CC
        s.PC_SW = s.PC_CB + CC
        s.PC_FNW = s.PC_SW + 3 * SCC
        s.NPC = s.PC_FNW + KC
        s.PR_DTB = 0
        s.PR_ALOG = s.HEADS
        s.PR_DSK = 2 * s.HEADS
        s.PR_NW = 3 * s.HEADS
        s.NPR = 3 * s.HEADS + s.DSSD


class Tile:
    def __init__(self, name, handle, space, base, dtype):
        self.name, self.h, self.space, self.base, self.dtype = name, handle, space, base, dtype
        self.ap = handle.ap()

    def __getitem__(self, idx):
        return V(self, self.ap[idx], _esz(self.dtype))

    def v(self):
        return V(self, self.ap, _esz(self.dtype))


class V:
    def __init__(self, tile, ap, esz):
        self.tile, self.ap, self.esz = tile, ap, esz

    def bf16(self):
        return V(self.tile, self.ap.bitcast(BF16), 2)

    def __getitem__(self, idx):
        return V(self.tile, self.ap[idx], self.esz)

    def re(self, pat, **kw):
        return V(self.tile, self.ap.rearrange(pat, **kw), self.esz)

    def rng(self):
        ap = self.ap
        dims = list(ap.ap)
        off = int(ap.offset)
        if self.tile.space in ("sb", "ps"):
            pstride = int(dims[0][0])
            free = dims[1:]
            lo = off % pstride if pstride > 0 else off
        else:
            free = dims
            lo = off
        ext = 1
        for st, cnt in free:
            ext += (int(cnt) - 1) * abs(int(st))
        return (self.tile.space, self.tile.base + lo * self.esz, self.tile.base + (lo + ext) * self.esz)


class _Rec:
    def __init__(self):
        self.calls = []

    def __getattr__(self, name):
        def f(*a, **kw):
            self.calls.append((name, a, kw))
            return self
        return f


class Prog:
    ENG = ["pe", "act", "dve", "pool", "sp"]
    BLK = {"pe": "tensor", "act": "scalar", "dve": "vector", "pool": "gpsimd", "sp": "sync"}
    KDMA = 8

    def __init__(self, nc):
        self.nc = nc
        self.ops = []
        self.recs = {}
        self.sb_top = SB_LO
        self.uid = 0

    def sb(self, name, shape, dtype, at=None):
        nbytes = int(np.prod(shape[1:])) * _esz(dtype)
        if at is None:
            at = (self.sb_top + 63) // 64 * 64
            self.sb_top = at + nbytes
        assert at + nbytes <= SB_HI, f"SBUF overflow allocating {name}: {at}+{nbytes}"
        self.uid += 1
        h = self.nc.alloc_sbuf_tensor_at(f"{name}_{self.uid}", list(shape), dtype, offset=at)
        return Tile(name, h, "sb", at, dtype)

    def mark(self):
        return self.sb_top

    def release(self, m):
        self.sb_top = m

    def dram(self, name, shape, dtype, kind):
        h = self.nc.dram_tensor(name, list(shape), dtype, kind=kind)
        return Tile(name, h, "d:" + name, 0, dtype)

    def op(self, eng, emit, reads=(), writes=(), kind="c"):
        i = len(self.ops)
        deps = set()
        for v in reads:
            sp, lo, hi = v.rng()
            L = self.recs.setdefault(sp, [])
            for r in L:
                if r[3] and r[0] < hi and lo < r[1]:
                    deps.add(r[2])
        for v in writes:
            sp, lo, hi = v.rng()
            L = self.recs.setdefault(sp, [])
            for r in L:
                if r[0] < hi and lo < r[1]:
                    deps.add(r[2])
        for v in reads:
            sp, lo, hi = v.rng()
            L = self.recs[sp]
            if kind == "c":
                for r in L:
                    if (not r[3]) and r[0] == lo and r[1] == hi and self.ops[r[2]]["eng"] == eng \
                            and self.ops[r[2]]["kind"] == "c":
                        r[2] = i
                        break
                else:
                    L.append([lo, hi, i, False])
            else:
                L.append([lo, hi, i, False])
        for v in writes:
            sp, lo, hi = v.rng()
            L = self.recs[sp]
            L[:] = [r for r in L if not (lo <= r[0] and r[1] <= hi)]
            L.append([lo, hi, i, True])
        deps.discard(i)
        rec = _Rec()
        emit(rec)
        self.ops.append(dict(eng=eng, calls=rec.calls, deps=deps, kind=kind))
        return i

    def finalize(self):
        nc = self.nc
        ops = self.ops
        dma_count = {e: 0 for e in self.ENG}
        dma_hist = {e: [] for e in self.ENG}
        for i, o in enumerate(ops):
            if o["kind"] == "d":
                e = o["eng"]
                n = dma_count[e]
                o["dma_n"] = n
                if n >= self.KDMA:
                    o["deps"].add(dma_hist[e][n - self.KDMA])
                dma_hist[e].append(i)
                dma_count[e] = n + 1
        need = [False] * len(ops)
        for i, o in enumerate(ops):
            for j in o["deps"]:
                pj = ops[j]
                if pj["kind"] == "c":
                    if pj["eng"] == "pe" and o["eng"] == "pe" and o["kind"] == "c":
                        continue
                    need[j] = True
        per_eng0 = {e: [i for i, o in enumerate(ops) if o["eng"] == e] for e in self.ENG}
        for e in ("pe", "act", "dve"):
            cl = [i for i in per_eng0[e] if ops[i]["kind"] == "c"]
            if cl:
                need[cl[-1]] = True
        sem_done = nc.alloc_semaphore("all_done")
        n_others = sum(1 for e in ("pe", "act", "dve", "sp") if per_eng0[e])
        sem_c = {e: nc.alloc_semaphore(f"prog_{e}") for e in self.ENG}
        sem_d = {e: [nc.alloc_semaphore(f"dma_{e}_{k}") for k in range(self.KDMA)] for e in ("sp", "pool", "act")}
        sem_cc = nc.alloc_semaphore("cc")
        cnt_c = {e: 0 for e in self.ENG}
        cnt_cc = 0
        tok = [None] * len(ops)
        for i, o in enumerate(ops):
            if o["kind"] == "c":
                if need[i]:
                    cnt_c[o["eng"]] += 1
                    tok[i] = (sem_c[o["eng"]], cnt_c[o["eng"]], 1)
            elif o["kind"] == "d":
                n = o["dma_n"]
                tok[i] = (sem_d[o["eng"]][n % self.KDMA], 16 * (n // self.KDMA + 1), 16)
            else:
                cnt_cc += 1
                tok[i] = (sem_cc, cnt_cc, 1)
        per_eng = {e: [] for e in self.ENG}
        for i, o in enumerate(ops):
            per_eng[o["eng"]].append(i)
        self.stats = {e: len(per_eng[e]) for e in self.ENG}

        def body(eng, e):
            waited = {}
            for i in per_eng[eng]:
                o = ops[i]
                ws = {}
                for j in o["deps"]:
                    pj = ops[j]
                    if pj["kind"] == "c" and pj["eng"] == "pe" and eng == "pe" and o["kind"] == "c":
                        continue
                    t = tok[j]
                    assert t is not None, (i, j)
                    key = id(t[0])
                    if key not in ws or ws[key][1] < t[1]:
                        ws[key] = t
                for key, t in ws.items():
                    if waited.get(key, 0) < t[1]:
                        e.wait_ge(t[0], t[1])
                        waited[key] = t[1]
                ins = None
                for (name, a, kw) in o["calls"]:
                    ins = getattr(e, name)(*a, **kw)
                if tok[i] is not None:
                    if o["kind"] == "cc":
                        ins.then_inc(tok[i][0])
                    else:
                        ins.then_inc(tok[i][0], tok[i][2])
            if eng in sem_d:
                n = dma_count[eng]
                for k in range(min(n, self.KDMA)):
                    total = 16 * ((n - 1 - k) // self.KDMA + 1)
                    e.wait_ge(sem_d[eng][k], total)
            if eng in ("pe", "act", "dve"):
                if cnt_c[eng] > 0:
                    e.wait_ge(sem_c[eng], cnt_c[eng])
            if eng != "pool":
                e.nop().then_inc(sem_done, 1)
            else:
                e.wait_ge(sem_done, n_others)

        with nc.Block(no_gpsimd_drain=True) as block:
            for eng in self.ENG:
                if per_eng[eng]:
                    getattr(block, self.BLK[eng])(lambda e, eng=eng: body(eng, e))


class _Stop(Exception):
    pass


def build_program(cfg, stop_at=None):
    c = cfg

    def chk(name):
        if stop_at == name:
            raise _Stop()
    nc = bass.Bass("TRN2", target_bir_lowering=False)
    P = Prog(nc)
    D, T, KC, NT, NH, NCH, TP = c.D, c.T, c.KC, c.NT, c.NH, c.NCH, c.TP
    HEADS, G = c.HEADS, c.G

    xT = P.dram("xT", [D, T], F32, "ExternalInput")
    c_col = P.dram("c_col", [128, KC], F32, "ExternalInput")
    w_cond = P.dram("w_cond", [D, c.RANK], F32, "ExternalInput")
    b_cond = P.dram("b_cond_col", [128, c.RC], F32, "ExternalInput")
    w_mod = P.dram("w_mod", [c.DEPTH, c.RANK, 6 * D], F32, "ExternalInput")
    pcol = P.dram("pcol", [c.DEPTH, 128, c.NPC], F32, "ExternalInput")
    prow = P.dram("prow", [c.DEPTH, c.NPR], F32, "ExternalInput")
    w_in = P.dram("w_in", [c.DEPTH, D, c.INC], F32, "ExternalInput")
    w_gate = P.dram("w_gate", [c.DEPTH, 2, c.GB, c.GBD, c.GBD], F32, "ExternalInput")
    w_br_ssd = P.dram("w_br_ssd", [c.DEPTH, c.DSSD, D], F32, "ExternalInput")
    w_br_sc = P.dram("w_br_sc", [c.DEPTH, c.DSC, D], F32, "ExternalInput")
    w_o = P.dram("w_o", [c.DEPTH, D, D], F32, "ExternalInput")
    w_ffn_in = P.dram("w_ffn_in", [c.DEPTH, D, 2 * c.DFF], F32, "ExternalInput")
    w_ffn_out = P.dram("w_ffn_out", [c.DEPTH, c.DFF, D], F32, "ExternalInput")
    cst_d = P.dram("cst", [128, 512], F32, "ExternalInput")
    cmeta_d = P.dram("cmeta", [128, 32], F32, "ExternalInput")
    outT = P.dram("outT", [D, T], F32, "ExternalOutput")
    xres = P.dram("xres", [D, T], F32, "Internal")
    yT_scr = P.dram("yT_scr", [c.DSSD, T], BF16, "Internal")
    NHC = KC * 3
    cc_h_in = [P.dram(f"cc_h_in{i}", [128, NHC], F32, "Internal") for i in range(c.DEPTH)]
    cc_h_out = [P.dram(f"cc_h_out{i}", [NCORES * 128, NHC], F32, "Internal") for i in range(c.DEPTH)]
    SW = 392
    cc_s_in = [[P.dram(f"cc_s_in{i}_{g}", [128, SW], F32, "Internal") for g in range(G)] for i in range(c.DEPTH)]
    cc_s_out = [[P.dram(f"cc_s_out{i}_{g}", [NCORES * 128, SW], F32, "Internal") for g in range(G)]
                for i in range(c.DEPTH)]

    def xrows(t):
        return t.v().re("(k p) t -> p k t", p=128)

    PS = []
    for b in range(8):
        h = nc.alloc_psum_tensor(f"psb{b}", [128, 512], F32)
        PS.append(Tile(f"ps{b}", h, "ps", b * 2048, F32))

    cst = P.sb("cst", [128, 512], F32)
    ident_b = P.sb("ident_b", [128, 128], BF16)
    cmeta = P.sb("cmeta", [128, 32], F32)
    omm = P.sb("omm", [128, 8], F32)
    pcs_t = P.sb("pcs", [128, c.NPC], F32)
    bmod_all = P.sb("bmod_all", [128, c.DEPTH, 6 * KC], F32)
    modt = P.sb("modt", [128, c.DEPTH, 6 * KC], F32)
    der = P.sb("der", [128, 2, KC], F32)
    epsc = P.sb("epsc", [128, 1], F32)
    hT = P.sb("hT", [128, KC, TP], BF16)
    WBYTES = 49152
    wring_lo = (P.sb_top + 63) // 64 * 64
    P.sb_top = wring_lo + WBYTES
    wstate = {"pos": wring_lo}
    U = cst[:, 128:256]
    SL = cst[:, 256:384]
    ONES = cst[:, 384:512]
    IDF = cst[:, 0:128]
    HMASK = cmeta[:, 16:17]

    def walloc(name, shape):
        nbytes = int(np.prod(shape[1:])) * 2
        nbytes = (nbytes + 63) // 64 * 64
        assert nbytes <= WBYTES
        if wstate["pos"] + nbytes > wring_lo + WBYTES:
            wstate["pos"] = wring_lo
        t = P.sb(name, shape, BF16, at=wstate["pos"])
        wstate["pos"] += nbytes
        return t

    def wload(dst_v, src_v):
        P.op("pool", lambda e: e.dma_start(out=dst_v.ap, in_=src_v.ap), reads=[], writes=[dst_v], kind="d")

    def dma(dst_v, src_v, rd=True):
        P.op("sp", lambda e: e.dma_start(out=dst_v.ap, in_=src_v.ap), reads=[src_v] if rd else [],
             writes=[dst_v], kind="d")

    def act(out, in_, func, bias=None, scale=None, accum=None, reads=()):
        def emit(e):
            kw = {}
            if bias is not None:
                kw["bias"] = bias.ap if isinstance(bias, V) else bias
            if scale is not None:
                kw["scale"] = scale.ap if isinstance(scale, V) else scale
            if accum is not None:
                kw["accum_out"] = accum.ap
            return e.activation(out=out.ap, in_=in_.ap, func=func, **kw)
        rs = [in_] + [x for x in (bias, scale) if isinstance(x, V)] + list(reads)
        ws = [out] + ([accum] if accum is not None else [])
        P.op("act", emit, reads=rs, writes=ws)

    def tt(out, a, b, op, eng="dve", b_ap=None, a_ap=None):
        P.op(eng, lambda e: e.tensor_tensor(out=out.ap, in0=(a_ap if a_ap is not None else a.ap),
                                            in1=(b_ap if b_ap is not None else b.ap), op=op),
             reads=[a, b], writes=[out])

    def ts(out, a, s1, s2, op0, op1=None, eng="dve"):
        def emit(e):
            kw = {}
            if op1 is not None:
                kw["op1"] = op1
            return e.tensor_scalar(out=out.ap, in0=a.ap, scalar1=(s1.ap if isinstance(s1, V) else s1),
                                   scalar2=(s2.ap if isinstance(s2, V) else s2), op0=op0, **kw)
        rs = [a] + [x for x in (s1, s2) if isinstance(x, V)]
        P.op(eng, emit, reads=rs, writes=[out])

    def stt(out, a, s, b, op0, op1):
        P.op("dve", lambda e: e.scalar_tensor_tensor(out=out.ap, in0=a.ap, scalar=(s.ap if isinstance(s, V) else s),
                                                     in1=b.ap, op0=op0, op1=op1),
             reads=[a, b] + ([s] if isinstance(s, V) else []), writes=[out])

    def copy(out, in_, eng="dve"):
        if eng == "act":
            act(out, in_, AF.Copy)
        else:
            P.op(eng, lambda e: e.tensor_copy(out=out.ap, in_=in_.ap), reads=[in_], writes=[out])

    def memset(v, val, eng="dve"):
        P.op(eng, lambda e: e.memset(v.ap, val), reads=[], writes=[v])

    def mm_group(outs, reads, fn):
        P.op("pe", fn, reads=reads, writes=outs)

    def bcast_last(v, n):
        ap = v.ap
        return ap.unsqueeze(2).to_broadcast([ap.shape[0], ap.shape[1], n])

    def bcast_mid(v, n):
        ap = v.ap
        return ap.unsqueeze(1).to_broadcast([ap.shape[0], n, ap.shape[1]])

    try:
        dma(cst.v(), cst_d.v(), rd=False)
        dma(cmeta.v(), cmeta_d.v(), rd=False)
        for i in range(c.DEPTH):
            dma(bmod_all[:, i, :], pcol[i][:, c.PC_BMOD:c.PC_BMOD + 6 * KC], rd=False)
        copy(ident_b.v(), IDF)
        memset(epsc.v(), EPS)
        ts(omm.v(), cmeta[:, 0:8], -1.0, 1.0, ALU.mult, ALU.add)

        m0 = P.mark()
        ccol = P.sb("ccol", [128, KC], F32)
        scol = P.sb("scol", [128, KC], F32)
        bcnd = P.sb("bcnd", [128, c.RC], F32)
        tT = P.sb("tT", [128, c.RC], F32)
        wc = P.sb("wc", [128, KC, c.RANK], F32)
        dma(ccol.v(), c_col.v(), rd=False)
        dma(bcnd.v(), b_cond.v(), rd=False)
        dma(wc.v(), w_cond.v().re("(k p) r -> p k r", p=128), rd=False)
        act(scol.v(), ccol.v(), AF.Silu)
        for rc in range(c.RC):
            def fn(e, rc=rc):
                ins = None
                for k in range(KC):
                    ins = e.matmul(PS[0][:, rc:rc + 1].ap, lhsT=wc[:, k, rc * 128:(rc + 1) * 128].ap,
                                   rhs=scol[:, k:k + 1].ap, start=(k == 0), stop=(k == KC - 1))
                return ins
            mm_group([PS[0][:, rc:rc + 1]], [wc.v(), scol.v()], fn)
        tt(tT.v(), PS[0][:, 0:c.RC], bcnd.v(), ALU.add)
        MB = 1024
        nblk = 6 * D // MB
        wm = [P.sb(f"wm{j}", [128, c.RC, MB], F32) for j in range(2)]
        for i in range(c.DEPTH):
            for bl in range(nblk):
                w = wm[(i * nblk + bl) % 2]
                dma(w.v(), w_mod[i][:, bl * MB:(bl + 1) * MB].re("(r p) n -> p r n", p=128), rd=False)
                nch = MB // 128
                def fn(e, w=w, bl=bl, nch=nch):
                    ins = None
                    for j in range(nch):
                        col = bl * nch + j
                        for rc in range(c.RC):
                            ins = e.matmul(PS[1][:, col:col + 1].ap, lhsT=w[:, rc, j * 128:(j + 1) * 128].ap,
                                           rhs=tT[:, rc:rc + 1].ap, start=(rc == 0), stop=(rc == c.RC - 1))
                    return ins
                mm_group([PS[1][:, bl * nch:(bl + 1) * nch]], [w.v(), tT.v()], fn)
            tt(modt[:, i, :], PS[1][:, 0:6 * KC], bmod_all[:, i, :], ALU.add)
        P.release(m0)
        chk("pro")

        def norm_stage(src, A, B, layer, halo_tile):
            m = P.mark()
            xc = [P.sb(f"xc{j}", [128, TP], F32) for j in range(2)]
            sq = [P.sb(f"sq{j}", [128, TP], F32) for j in range(2)]
            rs = P.sb("rs", [128, TP], F32)
            sv = xrows(src)
            H0 = HP - 3 if halo_tile is not None else HP
            for k in range(KC):
                x_ = xc[k % 2]
                dma(x_[:, HP:TP], sv[:, k, :])
                if halo_tile is not None:
                    copy(x_[:, HP - 3:HP], halo_tile[:, k * 3:(k + 1) * 3], eng="act")
                s_ = sq[k % 2]
                act(s_[:, H0:TP], x_[:, H0:TP], AF.Square)
                outs = [PS[h][:, 0:NT] for h in range(NH)]
                if halo_tile is not None:
                    outs.append(PS[2][:, 0:3])
                def fn(e, k=k, s_=s_):
                    ins = None
                    for h in range(NH):
                        ins = e.matmul(PS[h][:, 0:NT].ap, lhsT=ONES.ap, rhs=s_[:, HP + h * NT:HP + (h + 1) * NT].ap,
                                       start=(k == 0), stop=(k == KC - 1))
                    if halo_tile is not None:
                        ins = e.matmul(PS[2][:, 0:3].ap, lhsT=ONES.ap, rhs=s_[:, HP - 3:HP].ap,
                                       start=(k == 0), stop=(k == KC - 1))
                    return ins
                mm_group(outs, [s_[:, H0:TP]], fn)
            for h in range(NH):
                act(rs[:, HP + h * NT:HP + (h + 1) * NT], PS[h][:, 0:NT], AF.Sqrt, bias=epsc.v(), scale=1.0 / D)
            if halo_tile is not None:
                act(rs[:, HP - 3:HP], PS[2][:, 0:3], AF.Sqrt, bias=epsc.v(), scale=1.0 / D)
            P.op("dve", lambda e: e.reciprocal(out=rs[:, H0:TP].ap, in_=rs[:, H0:TP].ap), reads=[rs[:, H0:TP]],
                 writes=[rs[:, H0:TP]])
            for k in range(KC):
                x_ = xc[k % 2]
                dma(x_[:, HP:TP], sv[:, k, :])
                if halo_tile is not None:
                    copy(x_[:, HP - 3:HP], halo_tile[:, k * 3:(k + 1) * 3], eng="act")
                s_ = sq[k % 2]
                tt(s_[:, H0:TP], x_[:, H0:TP], rs[:, H0:TP], ALU.mult)
                act(hT[:, k, H0:TP], s_[:, H0:TP], AF.Identity, bias=B[:, k:k + 1], scale=A[:, k:k + 1])
            P.release(m)

        def final_norm_stage():
            m = P.mark()
            xc = [P.sb(f"fxc{j}", [128, T], F32) for j in range(2)]
            sq = [P.sb(f"fsq{j}", [128, T], F32) for j in range(2)]
            rs = P.sb("frs", [128, T], F32)
            sv = xrows(xres)
            ov = xrows(outT)
            for k in range(KC):
                x_ = xc[k % 2]
                dma(x_.v(), sv[:, k, :])
                s_ = sq[k % 2]
                act(s_.v(), x_.v(), AF.Square)
                def fn(e, k=k, s_=s_):
                    ins = None
                    for h in range(NH):
                        ins = e.matmul(PS[h][:, 0:NT].ap, lhsT=ONES.ap, rhs=s_[:, h * NT:(h + 1) * NT].ap,
                                       start=(k == 0), stop=(k == KC - 1))
                    return ins
                mm_group([PS[h][:, 0:NT] for h in range(NH)], [s_.v()], fn)
            for h in range(NH):
                act(rs[:, h * NT:(h + 1) * NT], PS[h][:, 0:NT], AF.Sqrt, bias=epsc.v(), scale=1.0 / D)
            P.op("dve", lambda e: e.reciprocal(out=rs.ap, in_=rs.ap), reads=[rs.v()], writes=[rs.v()])
            for k in range(KC):
                x_ = xc[k % 2]
                dma(x_.v(), sv[:, k, :])
                s_ = sq[k % 2]
                stt(s_.v(), x_.v(), pcs_t[:, c.PC_FNW + k:c.PC_FNW + k + 1], rs.v(), ALU.mult, ALU.mult)
                dma(ov[:, k, :], s_.v())
            P.release(m)

        def halo_stage(layer, src):
            sv = xrows(src)
            dma(cc_h_in[layer].v().re("p (k t) -> p k t", t=3), sv[:, :, T - 3:T])
            P.op("pool", lambda e: e.collective_compute("AllGather", ALU.bypass, replica_groups=[list(range(NCORES))],
                                                        ins=[cc_h_in[layer].ap], outs=[cc_h_out[layer].ap]),
                 reads=[cc_h_in[layer].v()], writes=[cc_h_out[layer].v()], kind="cc")
            hg = P.sb("hg", [128, NCORES, NHC], F32)
            xh = P.sb("xh", [128, NHC], F32)
            dma(hg.v(), cc_h_out[layer].v().re("(r p) f -> p r f", p=128))
            ts(xh.v(), hg[:, 0, :], cmeta[:, 8:9], None, ALU.mult)
            for j in range(1, NCORES):
                stt(xh.v(), hg[:, j, :], cmeta[:, 8 + j:9 + j], xh.v(), ALU.mult, ALU.add)
            return xh

        def proj_fm(wt, col0, ps_set, halo):
            outs = [PS[ps_set[h]][:, 0:NT] for h in range(NH)]
            if halo:
                outs.append(PS[ps_set[NH]][:, 0:3])
            def fn(e):
                ins = None
                for k in range(KC):
                    for h in range(NH):
                        ins = e.matmul(PS[ps_set[h]][:, 0:NT].ap, lhsT=wt[:, k, col0:col0 + 128].ap,
                                       rhs=hT[:, k, HP + h * NT:HP + (h + 1) * NT].ap, start=(k == 0), stop=(k == KC - 1))
                    if halo:
                        ins = e.matmul(PS[ps_set[NH]][:, 0:3].ap, lhsT=wt[:, k, col0:col0 + 128].ap,
                                       rhs=hT[:, k, HP - 3:HP].ap, start=(k == 0), stop=(k == KC - 1))
                return ins
            mm_group(outs, [wt[:, :, col0:col0 + 128], hT.v()], fn)

        for L in range(c.DEPTH):
            src = xT if L == 0 else xres
            mL = P.mark()
            dma(pcs_t.v(), pcol[L], rd=False)
            mPR = P.mark()
            prs = P.sb("prs", [128, c.NPR], F32)
            negA = P.sb("negA", [128, HEADS], F32)
            dma(prs.v(), V(prow, prow[L].ap.partition_broadcast(128), 4), rd=False)
            act(negA.v(), prs[:, c.PR_ALOG:c.PR_ALOG + HEADS], AF.Exp)
            ts(negA.v(), negA.v(), -1.0, None, ALU.mult)
            stt(der[:, 0, :], modt[:, L, 1 * KC:2 * KC], 1.0, pcs_t[:, c.PC_NMW:c.PC_NMW + KC], ALU.add, ALU.mult)
            stt(der[:, 1, :], modt[:, L, 4 * KC:5 * KC], 1.0, pcs_t[:, c.PC_NFW:c.PC_NFW + KC], ALU.add, ALU.mult)
            SH_M = modt[:, L, 0 * KC:1 * KC]
            G_M = modt[:, L, 2 * KC:3 * KC]
            SH_F = modt[:, L, 3 * KC:4 * KC]
            G_F = modt[:, L, 5 * KC:6 * KC]

            xh = halo_stage(L, src)
            chk('halo')
            norm_stage(src, der[:, 0, :], SH_M, L, xh)
            chk('norm1')

            dt_tok = P.sb("dt_tok", [128, NCH, HEADS], F32)
            da_tok = P.sb("da_tok", [128, NCH, HEADS], F32)
            wdt = walloc("wdt", [128, KC, HEADS])
            wload(wdt.v(), w_in[L][:, c.ODT:c.ODT + HEADS].re("(k p) n -> p k n", p=128))
            m1 = P.mark()
            xb = P.sb("xb", [128, HEADS], F32)
            ab = P.sb("ab", [128, HEADS], F32)
            for ch in range(NCH):
                pb = PS[ch % 2]
                def fn(e, ch=ch, pb=pb):
                    ins = None
                    for k in range(KC):
                        ins = e.matmul(pb[:, 0:HEADS].ap, lhsT=hT[:, k, HP + ch * 128:HP + (ch + 1) * 128].ap,
                                       rhs=wdt[:, k, :].ap, start=(k == 0), stop=(k == KC - 1))
                    return ins
                mm_group([pb[:, 0:HEADS]], [wdt.v(), hT.v()], fn)
                tt(xb.v(), pb[:, 0:HEADS], prs[:, c.PR_DTB:c.PR_DTB + HEADS], ALU.add)
                act(ab.v(), xb.v(), AF.Abs)
                act(ab.v(), ab.v(), AF.Exp, scale=-1.0)
                ts(ab.v(), ab.v(), 1.0, None, ALU.add)
                act(ab.v(), ab.v(), AF.Ln)
                stt(dt_tok[:, ch, :], xb.v(), 0.0, ab.v(), ALU.max, ALU.add)
                tt(da_tok[:, ch, :], dt_tok[:, ch, :], negA.v(), ALU.mult)
            P.release(m1)
            chk('dt')

            mS = P.mark()
            xsT = P.sb("xsT", [128, 3, T], BF16)
            BT = P.sb("BT", [128, T], BF16)
            CT = P.sb("CT", [128, T], BF16)
            sz_tok = P.sb("sz_tok", [128, NCH, 384], BF16)
            xs_tok = P.sb("xs_tok", [128, NCH, 384], BF16)
            xd_tok = P.sb("xd_tok", [128, NCH, 384], BF16)
            xdd = P.sb("xdd", [128, 384], BF16)
            B_tok = P.sb("B_tok", [128, NCH, 128], BF16)
            ex_all = P.sb("ex_all", [128, NCH, 24], F32)
            xpre = [P.sb("xpre0", [128, TP], F32)]
            cacc = [P.sb("cacc0", [128, T], F32)]
            Hs = P.sb("Hs", [128, 384], F32)
            Hb = P.sb("Hb", [128, 384], BF16)
            dprod = P.sb("dprod", [128, 8], F32)
            memset(dprod.v(), 0.0)
            sgj = [P.sb(f"sgj{j}", [128, SW], F32) for j in range(2)]
            dmt = P.sb("dmt", [128, NCORES, 6], F32)
            lhs6 = P.sb("lhs6", [128, 6, 128], F32)
            E6 = P.sb("E6", [128, 6, 128], BF16)
            M6 = P.sb("M6", [128, 6, 128], BF16)
            cbm = P.sb("cbm", [128, 128], BF16)
            yoff = P.sb("yoff", [128, 384], F32)
            yv = P.sb("yv", [128, 384], F32)
            ytmp = P.sb("ytmp", [128, 384], F32)
            ynb = P.sb("ynb", [128, 384], BF16)
            ssq = P.sb("ssq", [128, 1], F32)
            dskg = P.sb("dskg", [128, 384], F32)
            yTc = [P.sb(f"yTc{j}", [128, 3, 128], BF16) for j in range(2)]
            cidx_ctr = [0]

            def conv_epilogue(ps_set, cidx, dst):
                j = 0
                cidx_ctr[0] += 1
                xp, ca = xpre[j], cacc[j]
                act(xp[:, HP - 3:HP], PS[ps_set[NH]][:, 0:3], AF.Identity, scale=HMASK)
                for h in range(NH):
                    act(xp[:, HP + h * NT:HP + (h + 1) * NT], PS[ps_set[h]][:, 0:NT], AF.Copy)
                cw = lambda tap: pcs_t[:, c.PC_CW + tap * c.CC + cidx:c.PC_CW + tap * c.CC + cidx + 1]
                ts(ca.v(), xp[:, HP - 3:HP - 3 + T], cw(0), None, ALU.mult)
                for tap in (1, 2, 3):
                    stt(ca.v(), xp[:, HP - 3 + tap:HP - 3 + tap + T], cw(tap), ca.v(), ALU.mult, ALU.add)
                act(dst, ca.v(), AF.Silu, bias=pcs_t[:, c.PC_CB + cidx:c.PC_CB + cidx + 1])

            def states_mm(ch):
                P.op("dve", lambda e, ch=ch: e.tensor_tensor(
                    out=xdd.v().re("p (r d) -> p r d", d=64).ap,
                    in0=xd_tok[:, ch, :].re("p (r d) -> p r d", d=64).ap,
                    in1=bcast_last(ex_all[:, ch, 16:22], 64), op=ALU.mult),
                    reads=[xd_tok[:, ch, :], ex_all[:, ch, 16:22]], writes=[xdd.v()])
                mm_group([PS[2][:, 0:384]], [B_tok[:, ch, :], xdd.v()],
                         lambda e, ch=ch: e.matmul(PS[2][:, 0:384].ap, lhsT=B_tok[:, ch, :].ap,
                                                   rhs=xdd.v().ap, start=True, stop=True))

            for g in range(G):
                wx = walloc("wx", [128, KC, 384])
                wload(wx.v(), w_in[L][:, c.OX + 384 * g:c.OX + 384 * (g + 1)].re("(k p) n -> p k n", p=128))
                wbc = walloc("wbc", [128, KC, 256])
                wload(wbc[:, :, 0:128], w_in[L][:, c.OB + 128 * g:c.OB + 128 * (g + 1)].re("(k p) n -> p k n", p=128))
                wload(wbc[:, :, 128:256], w_in[L][:, c.OC + 128 * g:c.OC + 128 * (g + 1)].re("(k p) n -> p k n", p=128))
                sets = [[0, 1, 2], [3, 4, 5]] if NH == 2 else [[0, 2], [3, 5]]
                si = 0
                for q in range(3):
                    proj_fm(wx, q * 128, sets[si % 2], True)
                    conv_epilogue(sets[si % 2], 3 * g + q, xsT[:, q, :])
                    si += 1
                proj_fm(wbc, 0, sets[si % 2], True)
                conv_epilogue(sets[si % 2], c.DSSD // 128 + g, BT.v())
                si += 1
                proj_fm(wbc, 128, sets[si % 2], True)
                conv_epilogue(sets[si % 2], (c.DSSD + c.GN) // 128 + g, CT.v())
                si += 1
                wz = walloc("wz", [128, KC, 384])
                wload(wz.v(), w_in[L][:, c.OZ + 384 * g:c.OZ + 384 * (g + 1)].re("(k p) n -> p k n", p=128))
                for ch in range(NCH):
                    pb = PS[6 + ch % 2]
                    def fn(e, ch=ch, pb=pb):
                        ins = None
                        for k in range(KC):
                            ins = e.matmul(pb[:, 0:384].ap, lhsT=hT[:, k, HP + ch * 128:HP + (ch + 1) * 128].ap,
                                           rhs=wz[:, k, :].ap, start=(k == 0), stop=(k == KC - 1))
                        return ins
                    mm_group([pb[:, 0:384]], [wz.v(), hT.v()], fn)
                    act(sz_tok[:, ch, :], pb[:, 0:384], AF.Silu)
                P.op("dve", lambda e, g=g: e.tensor_copy(
                    out=dskg.v().re("p (r d) -> p r d", d=64).ap,
                    in_=bcast_last(prs[:, c.PR_DSK + 6 * g:c.PR_DSK + 6 * g + 6], 64)),
                    reads=[prs[:, c.PR_DSK + 6 * g:c.PR_DSK + 6 * g + 6]], writes=[dskg.v()])

                chk('ssdp')
                for ch in range(NCH):
                    tsl = slice(ch * 128, (ch + 1) * 128)
                    pbx = PS[0].v().bf16()
                    def fn(e, ch=ch, tsl=tsl, pbx=pbx):
                        ins = None
                        for q in range(3):
                            ins = e.transpose(pbx[:, q * 128:(q + 1) * 128].ap, xsT[:, q, tsl].ap, ident_b.v().ap)
                        ins = e.transpose(pbx[:, 384:512].ap, BT[:, tsl].ap, ident_b.v().ap)
                        return ins
                    mm_group([pbx[:, 0:512]], [xsT[:, :, tsl], BT[:, tsl], ident_b.v()], fn)
                    copy(xs_tok[:, ch, :], pbx[:, 0:384], eng="act")
                    copy(B_tok[:, ch, :], pbx[:, 384:512], eng="act")
                    chk('p1a')
                    chk(f'q{ch}a')
                    dsl = slice(6 * g, 6 * g + 6)
                    P.op("dve", lambda e, ch=ch, dsl=dsl, pbx=pbx: e.tensor_tensor(
                        out=xd_tok[:, ch, :].re("p (r d) -> p r d", d=64).ap,
                        in0=xs_tok[:, ch, :].re("p (r d) -> p r d", d=64).ap,
                        in1=bcast_last(dt_tok[:, ch, dsl], 64), op=ALU.mult),
                        reads=[xs_tok[:, ch, :], dt_tok[:, ch, dsl]], writes=[xd_tok[:, ch, :]])
                    chk('p1b')
                    chk(f'q{ch}b')
                    def fn2(e, ch=ch, dsl=dsl):
                        e.matmul(PS[1][:, 0:6].ap, lhsT=U.ap, rhs=da_tok[:, ch, dsl].ap, start=True, stop=True)
                        e.matmul(PS[1][:, 8:14].ap, lhsT=ONES.ap, rhs=da_tok[:, ch, dsl].ap, start=True, stop=True)
                        return e.matmul(PS[1][:, 16:22].ap, lhsT=SL.ap, rhs=da_tok[:, ch, dsl].ap, start=True, stop=True)
                    mm_group([PS[1][:, 0:24]], [da_tok[:, ch, dsl], cst.v()], fn2)
                    act(ex_all[:, ch, 0:6], PS[1][:, 0:6], AF.Exp)
                    act(ex_all[:, ch, 8:14], PS[1][:, 8:14], AF.Exp)
                    act(ex_all[:, ch, 16:22], PS[1][:, 16:22], AF.Exp)
                    chk('p1c')
                    chk(f'q{ch}c')
                    states_mm(ch)
                    chk('p1d')
                    chk(f'q{ch}d')
                    if ch == 0:
                        copy(Hs.v(), PS[2][:, 0:384])
                        copy(dprod[:, 0:6], ex_all[:, ch, 8:14])
                    else:
                        P.op("dve", lambda e, ch=ch: e.tensor_tensor(
                            out=Hs.v().re("p (r d) -> p r d", d=64).ap, in0=Hs.v().re("p (r d) -> p r d", d=64).ap,
                            in1=bcast_last(ex_all[:, ch, 8:14], 64), op=ALU.mult),
                            reads=[Hs.v(), ex_all[:, ch, 8:14]], writes=[Hs.v()])
                        tt(Hs.v(), Hs.v(), PS[2][:, 0:384], ALU.add)
                        tt(dprod[:, 0:6], dprod[:, 0:6], ex_all[:, ch, 8:14], ALU.mult)
                    chk(f'p1e{ch}')
                chk('pass1')
                dma(cc_s_in[L][g][:, 0:384], Hs.v())
                dma(cc_s_in[L][g][:, 384:392], dprod.v())
                P.op("pool", lambda e, g=g: e.collective_compute(
                    "AllGather", ALU.bypass, replica_groups=[list(range(NCORES))],
                    ins=[cc_s_in[L][g].ap], outs=[cc_s_out[L][g].ap]),
                    reads=[cc_s_in[L][g].v()], writes=[cc_s_out[L][g].v()], kind="cc")
                memset(Hs.v(), 0.0)
                for j in range(NCORES - 1):
                    sj = sgj[j % 2]
                    dma(sj.v(), cc_s_out[L][g][j * 128:(j + 1) * 128, :])
                    ts(dmt[:, j, :], sj[:, 384:390], cmeta[:, j:j + 1], omm[:, j:j + 1], ALU.mult, ALU.add)
                    P.op("dve", lambda e, j=j: e.tensor_tensor(
                        out=Hs.v().re("p (r d) -> p r d", d=64).ap, in0=Hs.v().re("p (r d) -> p r d", d=64).ap,
                        in1=bcast_last(dmt[:, j, :], 64), op=ALU.mult),
                        reads=[Hs.v(), dmt[:, j, :]], writes=[Hs.v()])
                    stt(Hs.v(), sj[:, 0:384], cmeta[:, j:j + 1], Hs.v(), ALU.mult, ALU.add)

                chk('exch')
                for ch in range(NCH):
                    tsl = slice(ch * 128, (ch + 1) * 128)
                    dsl = slice(6 * g, 6 * g + 6)
                    copy(Hb.v(), Hs.v())
                    mm_group([PS[3][:, 0:128]], [BT[:, tsl], CT[:, tsl]],
                             lambda e, tsl=tsl: e.matmul(PS[3][:, 0:128].ap, lhsT=BT[:, tsl].ap, rhs=CT[:, tsl].ap,
                                                         start=True, stop=True))
                    tt(cbm.v(), PS[3][:, 0:128], U, ALU.mult)
                    for r in range(6):
                        ts(lhs6[:, r, :], SL, da_tok[:, ch, 6 * g + r:6 * g + r + 1], None, ALU.mult)
                    def fn3(e):
                        ins = None
                        for r in range(6):
                            pb = PS[4] if r < 4 else PS[5]
                            rr = r if r < 4 else r - 4
                            ins = e.matmul(pb[:, rr * 128:(rr + 1) * 128].ap, lhsT=lhs6[:, r, :].ap, rhs=U.ap,
                                           start=True, stop=True)
                        return ins
                    mm_group([PS[4][:, 0:512], PS[5][:, 0:256]], [lhs6.v(), cst.v()], fn3)
                    act(E6[:, 0:4, :], PS[4][:, 0:512].re("p (r l) -> p r l", l=128), AF.Exp)
                    act(E6[:, 4:6, :], PS[5][:, 0:256].re("p (r l) -> p r l", l=128), AF.Exp)
                    P.op("dve", lambda e: e.tensor_tensor(out=M6.v().ap, in0=E6.v().ap, in1=bcast_mid(cbm.v(), 6),
                                                          op=ALU.mult),
                         reads=[E6.v(), cbm.v()], writes=[M6.v()])
                    def fn4(e, ch=ch):
                        ins = None
                        for r in range(6):
                            ins = e.matmul(PS[6][:, r * 64:(r + 1) * 64].ap, lhsT=M6[:, r, :].ap,
                                           rhs=xd_tok[:, ch, r * 64:(r + 1) * 64].ap, start=True, stop=True)
                        return ins
                    mm_group([PS[6][:, 0:384]], [M6.v(), xd_tok[:, ch, :]], fn4)
                    mm_group([PS[7][:, 0:384]], [CT[:, tsl], Hb.v()],
                             lambda e, tsl=tsl: e.matmul(PS[7][:, 0:384].ap, lhsT=CT[:, tsl].ap, rhs=Hb.v().ap,
                                                         start=True, stop=True))
                    P.op("dve", lambda e, ch=ch: e.tensor_tensor(
                        out=yoff.v().re("p (r d) -> p r d", d=64).ap,
                        in0=PS[7][:, 0:384].re("p (r d) -> p r d", d=64).ap,
                        in1=bcast_last(ex_all[:, ch, 0:6], 64), op=ALU.mult),
                        reads=[PS[7][:, 0:384], ex_all[:, ch, 0:6]], writes=[yoff.v()])
                    tt(yv.v(), PS[6][:, 0:384], yoff.v(), ALU.add)
                    tt(ytmp.v(), xs_tok[:, ch, :], dskg.v(), ALU.mult)
                    tt(yv.v(), yv.v(), ytmp.v(), ALU.add)
                    tt(yv.v(), yv.v(), sz_tok[:, ch, :], ALU.mult)
                    memset(ssq.v(), 0.0)
                    act(ytmp.v(), yv.v(), AF.Square, accum=ssq.v())
                    act(ssq.v(), ssq.v(), AF.Sqrt, bias=epsc.v(), scale=1.0 / 384.0)
                    P.op("dve", lambda e: e.reciprocal(out=ssq.v().ap, in_=ssq.v().ap), reads=[ssq.v()], writes=[ssq.v()])
                    stt(ynb.v(), yv.v(), ssq.v(), prs[:, c.PR_NW + 384 * g:c.PR_NW + 384 * (g + 1)], ALU.mult, ALU.mult)
                    pbx = PS[0].v().bf16()
                    def fn5(e, pbx=pbx):
                        ins = None
                        for q in range(3):
                            ins = e.transpose(pbx[:, q * 128:(q + 1) * 128].ap, ynb[:, q * 128:(q + 1) * 128].ap,
                                              ident_b.v().ap)
                        return ins
                    mm_group([pbx[:, 0:384]], [ynb.v(), ident_b.v()], fn5)
                    yc = yTc[ch % 2]
                    copy(yc.v(), pbx[:, 0:384].re("p (q t) -> p q t", t=128), eng="act")
                    dma(yT_scr.v().re("(j p) t -> p j t", p=128)[:, 3 * g:3 * g + 3, tsl], yc.v())
                    states_mm(ch)
                    P.op("dve", lambda e, ch=ch: e.tensor_tensor(
                        out=Hs.v().re("p (r d) -> p r d", d=64).ap, in0=Hs.v().re("p (r d) -> p r d", d=64).ap,
                        in1=bcast_last(ex_all[:, ch, 8:14], 64), op=ALU.mult),
                        reads=[Hs.v(), ex_all[:, ch, 8:14]], writes=[Hs.v()])
                    tt(Hs.v(), Hs.v(), PS[2][:, 0:384], ALU.add)
            P.release(mS)
            P.release(mPR)
            chk('ssd')

            mM = P.mark()
            ybT = P.sb("ybT", [128, c.SCC, T], BF16)
            m2 = P.mark()
            vsb = P.sb("vsb", [128, TP], F32)
            usb = P.sb("usb", [128, TP], F32)
            csb = P.sb("csb", [128, TP], F32)
            sacc = P.sb("sacc", [128, T], F32)
            for j in range(c.SCC):
                wsc = walloc("wsc", [128, KC, 384])
                for q, o in enumerate((c.OV, c.OCG, c.OBG)):
                    wload(wsc[:, :, q * 128:(q + 1) * 128],
                          w_in[L][:, o + 128 * j:o + 128 * (j + 1)].re("(k p) n -> p k n", p=128))
                if NH == 2:
                    sv_, sc_, sb_ = [0, 1, 6], [2, 3, 7], [4, 5]
                else:
                    sv_, sc_, sb_ = [0, 6], [2, 7], [4]
                proj_fm(wsc, 0, sv_, True)
                proj_fm(wsc, 128, sc_, True)
                proj_fm(wsc, 256, sb_, False)
                act(vsb[:, HP - 3:HP], PS[sv_[NH]][:, 0:3], AF.Identity, scale=HMASK)
                for h in range(NH):
                    act(vsb[:, HP + h * NT:HP + (h + 1) * NT], PS[sv_[h]][:, 0:NT], AF.Copy)
                act(csb[:, HP - 3:HP], PS[sc_[NH]][:, 0:3], AF.Identity, scale=HMASK)
                for h in range(NH):
                    act(csb[:, HP + h * NT:HP + (h + 1) * NT], PS[sc_[h]][:, 0:NT], AF.Copy)
                tt(usb[:, HP - 3:TP], csb[:, HP - 3:TP], vsb[:, HP - 3:TP], ALU.mult)
                sw = lambda tap: pcs_t[:, c.PC_SW + tap * c.SCC + j:c.PC_SW + tap * c.SCC + j + 1]
                ts(sacc.v(), usb[:, HP - 2:HP - 2 + T], sw(0), None, ALU.mult)
                stt(sacc.v(), usb[:, HP - 1:HP - 1 + T], sw(1), sacc.v(), ALU.mult, ALU.add)
                stt(sacc.v(), usb[:, HP:HP + T], sw(2), sacc.v(), ALU.mult, ALU.add)
                for h in range(NH):
                    act(csb[:, HP + h * NT:HP + (h + 1) * NT], PS[sb_[h]][:, 0:NT], AF.Copy)
                    tt(ybT[:, j, h * NT:(h + 1) * NT], sacc[:, h * NT:(h + 1) * NT],
                       csb[:, HP + h * NT:HP + (h + 1) * NT], ALU.mult)
            P.release(m2)
            chk('sc')

            yTa = P.sb("yTa", [128, c.YC, NT], BF16)
            mgs = P.sb("mgs", [128, 2, NT], BF16)
            sgt = [P.sb(f"sgt{j}", [128, NT], F32) for j in range(4)]
            yae = P.sb("yae", [128, NT], F32)
            ybe = P.sb("ybe", [128, NT], F32)
            it = 0
            NWM = c.YC + c.SCC + 4
            for h in range(NH):
                hs = slice(h * NT, (h + 1) * NT)
                hhs = slice(HP + h * NT, HP + (h + 1) * NT)
                dma(yTa.v(), yT_scr.v().re("(j p) t -> p j t", p=128)[:, :, hs])
                for kb in range(c.GB):
                    wmg = walloc("wmg", [128, NWM, 256])
                    wload(wmg[:, 0:c.YC, :], w_br_ssd[L][:, kb * 256:(kb + 1) * 256].re("(k p) n -> p k n", p=128))
                    wload(wmg[:, c.YC:c.YC + c.SCC, :],
                          w_br_sc[L][:, kb * 256:(kb + 1) * 256].re("(k p) n -> p k n", p=128))
                    for gi in range(2):
                        wload(wmg[:, c.YC + c.SCC + 2 * gi:c.YC + c.SCC + 2 * gi + 2, :],
                              w_gate[L][gi][kb].re("(k p) n -> p k n", p=128))
                    for mm in range(2):
                        mchunk = 2 * kb + mm
                        cs = slice(mm * 128, (mm + 1) * 128)
                        base = 4 * (it % 2)
                        it += 1
                        def fn(e, cs=cs, base=base, kb=kb, wmg=wmg):
                            ins = None
                            for k in range(c.YC):
                                ins = e.matmul(PS[base][:, 0:NT].ap, lhsT=wmg[:, k, cs].ap, rhs=yTa[:, k, :].ap,
                                               start=(k == 0), stop=(k == c.YC - 1))
                            for k in range(c.SCC):
                                ins = e.matmul(PS[base + 1][:, 0:NT].ap, lhsT=wmg[:, c.YC + k, cs].ap,
                                               rhs=ybT[:, k, hs].ap, start=(k == 0), stop=(k == c.SCC - 1))
                            for gi in range(2):
                                for dc in range(2):
                                    ins = e.matmul(PS[base + 2 + gi][:, 0:NT].ap,
                                                   lhsT=wmg[:, c.YC + c.SCC + 2 * gi + dc, cs].ap,
                                                   rhs=hT[:, 2 * kb + dc, hhs].ap, start=(dc == 0), stop=(dc == 1))
                            return ins
                        mm_group([PS[base + q][:, 0:NT] for q in range(4)],
                                 [wmg.v(), yTa.v(), ybT[:, :, hs], hT[:, 2 * kb:2 * kb + 2, hhs]], fn)
                        s0, s1 = sgt[2 * (it % 2)], sgt[2 * (it % 2) + 1]
                        act(s0.v(), PS[base + 2][:, 0:NT], AF.Sigmoid,
                            bias=pcs_t[:, c.PC_BG + mchunk:c.PC_BG + mchunk + 1])
                        act(s1.v(), PS[base + 3][:, 0:NT], AF.Sigmoid,
                            bias=pcs_t[:, c.PC_BG + KC + mchunk:c.PC_BG + KC + mchunk + 1])
                        act(yae.v(), PS[base][:, 0:NT], AF.Copy)
                        act(ybe.v(), PS[base + 1][:, 0:NT], AF.Copy)
                        tt(s0.v(), s0.v(), yae.v(), ALU.mult)
                        tt(s1.v(), s1.v(), ybe.v(), ALU.mult)
                        tt(mgs[:, mm, :], s0.v(), s1.v(), ALU.add)
                    for mm in range(2):
                        copy(hT[:, 2 * kb + mm, hhs], mgs[:, mm, :], eng="act")
            P.release(mM)
            chk('merge')

            def resid_gemm(wt_fn, nk, rhs_fn, rhs_reads, gate_cols, src_x, tag):
                m = P.mark()
                xio = [P.sb(f"xio{tag}{j}", [128, T], F32) for j in range(3)]
                rtmp = [P.sb(f"rtmp{tag}{j}", [128, NT], F32) for j in range(2)]
                sv = xrows(src_x)
                dv = xrows(xres)
                it2 = 0
                WCOL = min(256, D)
                for n0 in range(0, D, WCOL):
                    wt = wt_fn(n0, WCOL)
                    for mm in range(WCOL // 128):
                        mchunk = n0 // 128 + mm
                        cs = slice(mm * 128, (mm + 1) * 128)
                        xt_ = xio[mchunk % 3]
                        dma(xt_.v(), sv[:, mchunk, :])
                        for h in range(NH):
                            pb = PS[it2 % 8]
                            it2 += 1
                            hs = slice(h * NT, (h + 1) * NT)
                            def fn(e, cs=cs, hs=hs, pb=pb, wt=wt):
                                ins = None
                                for k in range(nk):
                                    ins = e.matmul(pb[:, 0:NT].ap, lhsT=wt[:, k, cs].ap, rhs=rhs_fn(k, hs).ap,
                                                   start=(k == 0), stop=(k == nk - 1))
                                return ins
                            mm_group([pb[:, 0:NT]], [wt[:, :, cs]] + rhs_reads(hs), fn)
                            tr_ = rtmp[it2 % 2]
                            act(tr_.v(), pb[:, 0:NT], AF.Identity, scale=gate_cols[:, mchunk:mchunk + 1])
                            tt(xt_[:, hs], tr_.v(), xt_[:, hs], ALU.add)
                        dma(dv[:, mchunk, :], xt_.v())
                P.release(m)

            def wo_tile(n0, wcol):
                wt = walloc("wo", [128, KC, wcol])
                wload(wt.v(), w_o[L][:, n0:n0 + wcol].re("(k p) n -> p k n", p=128))
                return wt
            resid_gemm(wo_tile, KC, lambda k, hs: hT[:, k, HP + hs.start:HP + hs.stop], lambda hs: [hT.v()], G_M, src, "a")

            chk('wo')
            norm_stage(xres, der[:, 1, :], SH_F, L, None)
            nslabs = 4 if c.FC >= 16 else 2
            bounds = [round(i * c.FC / nslabs) for i in range(nslabs + 1)]
            for sl in range(nslabs):
                f_lo, f_hi = bounds[sl], bounds[sl + 1]
                ns = f_hi - f_lo
                mF = P.mark()
                aT = P.sb("aT", [128, ns, T], BF16)
                sgf = [P.sb(f"sgf{j}", [128, NT], F32) for j in range(2)]
                ugf = [P.sb(f"ugf{j}", [128, NT], F32) for j in range(2)]
                it3 = 0
                f = f_lo
                while f < f_hi:
                    nf = 1
                    wt = walloc("wfi", [128, KC, 2 * nf * 128])
                    wload(wt[:, :, 0:nf * 128], w_ffn_in[L][:, f * 128:(f + nf) * 128].re("(k p) n -> p k n", p=128))
                    wload(wt[:, :, nf * 128:2 * nf * 128],
                          w_ffn_in[L][:, c.DFF + f * 128:c.DFF + (f + nf) * 128].re("(k p) n -> p k n", p=128))
                    for ff in range(nf):
                        for h in range(NH):
                            pg, pu = PS[2 * (it3 % 4)], PS[2 * (it3 % 4) + 1]
                            it3 += 1
                            hhs = slice(HP + h * NT, HP + (h + 1) * NT)
                            gs = slice(ff * 128, (ff + 1) * 128)
                            us = slice(nf * 128 + ff * 128, nf * 128 + (ff + 1) * 128)
                            def fn(e, pg=pg, pu=pu, hhs=hhs, gs=gs, us=us, wt=wt):
                                ins = None
                                for k in range(KC):
                                    ins = e.matmul(pg[:, 0:NT].ap, lhsT=wt[:, k, gs].ap, rhs=hT[:, k, hhs].ap,
                                                   start=(k == 0), stop=(k == KC - 1))
                                for k in range(KC):
                                    ins = e.matmul(pu[:, 0:NT].ap, lhsT=wt[:, k, us].ap, rhs=hT[:, k, hhs].ap,
                                                   start=(k == 0), stop=(k == KC - 1))
                                return ins
                            mm_group([pg[:, 0:NT], pu[:, 0:NT]], [wt.v(), hT.v()], fn)
                            s_ = sgf[it3 % 2]
                            act(s_.v(), pg[:, 0:NT], AF.Silu)
                            u_ = ugf[it3 % 2]
                            act(u_.v(), pu[:, 0:NT], AF.Copy)
                            tt(aT[:, f - f_lo + ff, h * NT:(h + 1) * NT], s_.v(), u_.v(), ALU.mult)
                    f += nf

                def fo_tile(n0, wcol, f_lo=f_lo, f_hi=f_hi, ns=ns):
                    wt = walloc("wfo", [128, ns, wcol])
                    wload(wt.v(), w_ffn_out[L][f_lo * 128:f_hi * 128, n0:n0 + wcol].re("(k p) n -> p k n", p=128))
                    return wt
                resid_gemm(fo_tile, ns, lambda k, hs: aT[:, k, hs], lambda hs: [aT[:, :, hs]], G_F, xres, f"f{sl}")
                P.release(mF)
            chk('ffn')
            P.release(mL)

        final_norm_stage()
    except _Stop:
        pass
    P.finalize()
    return nc, P


_CACHE = {}


def _consts():
    cst = np.zeros((128, 512), np.float32)
    cst[:, 0:128] = np.eye(128, dtype=np.float32)
    s = np.arange(128)
    cst[:, 128:256] = (s[:, None] <= s[None, :]).astype(np.float32)
    cst[:, 256:384] = (s[:, None] > s[None, :]).astype(np.float32)
    cst[:, 384:512] = 1.0
    return cst


def _col(v, nchunk):
    return np.ascontiguousarray(np.asarray(v, np.float32).reshape(nchunk, 128).T)


def run(cfg, inputs):
    c = cfg
    key = (c.D, c.SEQ, c.DEPTH, c.G, c.GB, c.RANK)
    if key not in _CACHE:
        _CACHE[key] = build_program(c)
    nc, P = _CACHE[key]
    f = lambda n: np.ascontiguousarray(np.asarray(inputs[n], np.float32))
    x = f("x")[0]
    KC = c.KC
    pcol = np.zeros((c.DEPTH, 128, c.NPC), np.float32)
    prow = np.zeros((c.DEPTH, c.NPR), np.float32)
    b_mod, nmw, nfw, b_gate = f("b_mod"), f("norm_mix_w"), f("norm_ffn_w"), f("b_gate")
    cw, cb, sw, fnw = f("conv_ssd_w"), f("conv_ssd_b"), f("sc_conv_w"), f("final_norm_w")
    for i in range(c.DEPTH):
        pcol[i, :, c.PC_BMOD:c.PC_BMOD + 6 * KC] = _col(b_mod[i], 6 * KC)
        pcol[i, :, c.PC_NMW:c.PC_NMW + KC] = _col(nmw[i], KC)
        pcol[i, :, c.PC_NFW:c.PC_NFW + KC] = _col(nfw[i], KC)
        pcol[i, :, c.PC_BG:c.PC_BG + 2 * KC] = _col(b_gate[i].reshape(-1), 2 * KC)
        for tap in range(4):
            pcol[i, :, c.PC_CW + tap * c.CC:c.PC_CW + (tap + 1) * c.CC] = _col(cw[i, tap], c.CC)
        pcol[i, :, c.PC_CB:c.PC_CB + c.CC] = _col(cb[i], c.CC)
        for tap in range(3):
            pcol[i, :, c.PC_SW + tap * c.SCC:c.PC_SW + (tap + 1) * c.SCC] = _col(sw[i, tap], c.SCC)
        pcol[i, :, c.PC_FNW:c.PC_FNW + KC] = _col(fnw, KC)
        prow[i, c.PR_DTB:c.PR_DTB + c.HEADS] = f("dt_bias")[i]
        prow[i, c.PR_ALOG:c.PR_ALOG + c.HEADS] = f("a_log")[i]
        prow[i, c.PR_DSK:c.PR_DSK + c.HEADS] = f("d_skip")[i]
        prow[i, c.PR_NW:c.PR_NW + c.DSSD] = f("ssd_norm_w")[i]
    shared = {
        "c_col": _col(f("c")[0], KC),
        "w_cond": f("w_cond"),
        "b_cond_col": _col(f("b_cond"), c.RC),
        "w_mod": f("w_mod"),
        "pcol": pcol,
        "prow": prow,
        "w_in": f("w_in"),
        "w_gate": f("w_gate"),
        "w_br_ssd": f("w_br_ssd"),
        "w_br_sc": f("w_br_sc"),
        "w_o": f("w_o"),
        "w_ffn_in": f("w_ffn_in"),
        "w_ffn_out": f("w_ffn_out"),
        "cst": _consts(),
    }
    in_maps = []
    for k in range(NCORES):
        cm = np.zeros((128, 32), np.float32)
        for j in range(8):
            cm[:, j] = 1.0 if j < k else 0.0
            cm[:, 8 + j] = 1.0 if j == k - 1 else 0.0
        cm[:, 16] = 1.0 if k > 0 else 0.0
        d = dict(shared)
        d["xT"] = np.ascontiguousarray(x[k * c.T:(k + 1) * c.T, :].T)
        d["cmeta"] = cm
        in_maps.append(d)
    res = run_bass_kernel_spmd(nc, in_maps, core_ids=list(range(NCORES)))
    out = np.empty((1, c.SEQ, c.D), np.float32)
    for k in range(NCORES):
        out[0, k * c.T:(k + 1) * c.T, :] = res.results[k]["outT"].T
    return out


def kernel(**inputs):
    return run(Cfg(), inputs)
```
